# Optimizing a Trainium2 kernel written in Bass

```python
import math
import jax, jax.numpy as jnp
from jax import lax
import numpy as np

D_MODEL = 2048
BATCH = 1
SEQ = 16384
DEPTH = 4

CHUNK = 64
N_META = 16
SB_BLOCK = 128
PAD_FRONT = SB_BLOCK - N_META
PREFIX = PAD_FRONT + N_META

GLA_HEADS = 4
GLA_DK = D_MODEL // 2 // GLA_HEADS
GLA_DV = D_MODEL // GLA_HEADS
GLA_GATE_RANK = 16
GLA_TAU = 16.0
GLA_QK = GLA_HEADS * GLA_DK
GLA_V = GLA_HEADS * GLA_DV

SB_DH = 128
SB_HEADS = D_MODEL // 2 // SB_DH
SB_W = SB_HEADS * SB_DH

SSM_EXPAND = 2
D_INNER = SSM_EXPAND * D_MODEL
SSM_HEADDIM = 64
SSM_HEADS = D_INNER // SSM_HEADDIM
SSM_GROUPS = 8
SSM_HPG = SSM_HEADS // SSM_GROUPS
D_STATE = 128
CONV_K = 4
CONV_DIM = D_INNER + 2 * SSM_GROUPS * D_STATE

EVEN_SPLITS = (GLA_QK, GLA_QK, GLA_V, GLA_V, GLA_GATE_RANK, SB_W, SB_W, SB_W, SB_W)
EVEN_IN = sum(EVEN_SPLITS)
EVEN_MIX = GLA_V + SB_W
ODD_SPLITS = (D_INNER, CONV_DIM, SSM_HEADS)
ODD_IN = sum(ODD_SPLITS)

DEEPNORM_ALPHA = (2 * DEPTH) ** 0.25
DEEPNORM_BETA = (8 * DEPTH) ** -0.25
LN_EPS = 1e-5
RMS_EPS = 1e-6

kernel_name = 'hybrid_gla_stickbreak_ssd_deepnorm'


def _split(a, sizes):
    return jnp.split(a, np.cumsum(sizes)[:-1].tolist(), axis=-1)


def _layer_norm(x, g, b):
    mu = jnp.mean(x, -1, keepdims=True)
    xc = x - mu
    var = jnp.mean(xc * xc, -1, keepdims=True)
    return xc * lax.rsqrt(var + LN_EPS) * g + b


def _rms_norm_groups(y, w, n_groups):
    shp = y.shape
    yg = y.reshape(*shp[:-1], n_groups, shp[-1] // n_groups)
    yg = yg * lax.rsqrt(jnp.mean(yg * yg, -1, keepdims=True) + RMS_EPS)
    return yg.reshape(shp) * w


def _to_chunks(a):
    return a.reshape(a.shape[0], a.shape[1] // CHUNK, CHUNK, *a.shape[2:])


def _causal_depthwise_conv(u, w, b):
    out = lax.conv_general_dilated(
        u, w.astype(u.dtype)[:, None, :], window_strides=(1,),
        padding=((CONV_K - 1, 0),), dimension_numbers=('NWC', 'WIO', 'NWC'),
        feature_group_count=u.shape[-1])
    return out + b


def _gla(q, k, v, log_a):
    bsz, t_len, n_h, d_k = q.shape
    d_v = v.shape[-1]
    q, k, v, log_a = (_to_chunks(t) for t in (q, k, v, log_a))
    g_cum = jnp.cumsum(log_a, axis=2)
    g_end = g_cum[:, :, -1]
    q_dec = q * jnp.exp(g_cum)
    k_inv = k * jnp.exp(-g_cum)
    k_end = k * jnp.exp(g_end[:, :, None] - g_cum)
    causal = jnp.tril(jnp.ones((CHUNK, CHUNK), dtype=bool))
    scores = jnp.einsum('bcthk,bcshk->bchts', q_dec, k_inv)
    scores = jnp.where(causal, scores, 0.0)
    o_intra = jnp.einsum('bchts,bcshv->bcthv', scores, v)

    def step(state, inp):
        qd, ke, vv, dec = inp
        o = jnp.einsum('bthk,bhkv->bthv', qd, state)
        state = dec[..., None] * state + jnp.einsum('bshk,bshv->bhkv', ke, vv)
        return state, o

    xs = tuple(jnp.moveaxis(t, 1, 0) for t in (q_dec, k_end, v, jnp.exp(g_end)))
    s0 = jnp.zeros((bsz, n_h, d_k, d_v), jnp.float32)
    _, o_inter = lax.scan(step, s0, xs)
    o = o_intra + jnp.moveaxis(o_inter, 0, 1)
    return o.reshape(bsz, t_len, n_h, d_v)


def _stick_breaking(q, k, v, valid):
    bsz, t_len, n_h, d_h = q.shape
    q = jnp.transpose(q, (0, 2, 1, 3)) * (d_h ** -0.5)
    k = jnp.transpose(k, (0, 2, 1, 3))
    v = jnp.transpose(v, (0, 2, 1, 3))
    blk = jnp.arange(SB_BLOCK)
    later_in_block = (blk[:, None] > blk[None, :]).astype(jnp.float32)
    outs = []
    for i in range(t_len // SB_BLOCK):
        n_k = i + 1
        k_len = n_k * SB_BLOCK
        qb = q[:, :, i * SB_BLOCK:(i + 1) * SB_BLOCK]
        z = jnp.einsum('bhqd,bhkd->bhqk', qb, k[:, :, :k_len]).astype(jnp.float32)
        q_pos = i * SB_BLOCK + blk
        mask = (jnp.arange(k_len)[None, :] < q_pos[:, None]) & valid[None, :k_len]
        log_stay = jnp.where(mask, jax.nn.log_sigmoid(-z), 0.0)
        ls = log_stay.reshape(bsz, n_h, SB_BLOCK, n_k, SB_BLOCK)
        within = jnp.einsum('bhqnk,kl->bhqnl', ls, later_in_block)
        tot = jnp.sum(ls, axis=-1)
        later = lax.cumsum(tot, axis=3, reverse=True) - tot
        log_between = (within + later[..., None]).reshape(bsz, n_h, SB_BLOCK, k_len)
        w = jnp.where(mask, jnp.exp(jax.nn.log_sigmoid(z) + log_between), 0.0)
        outs.append(jnp.einsum('bhqk,bhkd->bhqd', w, v[:, :, :k_len].astype(jnp.float32)))
    out = jnp.concatenate(outs, axis=2)
    return jnp.transpose(out, (0, 2, 1, 3))


def _ssd(x, b_in, c_in, dt, a_log, d_skip):
    bsz, t_len = x.shape[:2]
    a_neg = -jnp.exp(a_log).reshape(SSM_GROUPS, SSM_HPG)
    xdt = x * dt[..., None]
    x_c, xdt, b_c, c_c, a_c = (_to_chunks(t) for t in (x, xdt, b_in, c_in, dt * a_neg))
    a_cum = jnp.cumsum(a_c, axis=2)
    a_end = a_cum[:, :, -1]
    causal = jnp.tril(jnp.ones((CHUNK, CHUNK), dtype=bool))
    seg = a_cum[:, :, :, None] - a_cum[:, :, None]
    l_mat = jnp.exp(jnp.where(causal[:, :, None, None], seg, -jnp.inf))
    cb = jnp.einsum('bctgn,bcsgn->bctsg', c_c, b_c)
    y_diag = jnp.einsum('bctsgh,bcsghp->bctghp', l_mat * cb[..., None], xdt)
    decay_s = jnp.exp(a_end[:, :, None] - a_cum)

    def step(hs, inp):
        c_t, in_dec, b_s, dec_s, xd, dec_chunk = inp
        y = jnp.einsum('btgn,bghpn->btghp', c_t, hs) * in_dec[..., None]
        hs = dec_chunk[..., None, None] * hs + jnp.einsum('bsgn,bsgh,bsghp->bghpn', b_s, dec_s, xd)
        return hs, y

    xs = tuple(jnp.moveaxis(t, 1, 0) for t in (c_c, jnp.exp(a_cum), b_c, decay_s, xdt, jnp.exp(a_end)))
    h0 = jnp.zeros((bsz, SSM_GROUPS, SSM_HPG, SSM_HEADDIM, D_STATE), jnp.float32)
    _, y_off = lax.scan(step, h0, xs)
    y = y_diag + jnp.moveaxis(y_off, 0, 1) + d_skip.reshape(SSM_GROUPS, SSM_HPG)[..., None] * x_c
    return y.reshape(bsz, t_len, D_INNER)


def _even_mixer(h, valid, w_in, w_gate2, b_gate, norm_w, w_out):
    bsz, t_len, _ = h.shape
    vmask = valid[None, :, None]
    proj = jnp.einsum('btd,de->bte', h, w_in).astype(jnp.float32)
    gq, gk, gv, gr, gu, sq, sk, sv, sr = _split(proj, EVEN_SPLITS)
    gq = gq.reshape(bsz, t_len, GLA_HEADS, GLA_DK) * (GLA_DK ** -0.5)
    gk = jnp.where(vmask, gk, 0.0).reshape(bsz, t_len, GLA_HEADS, GLA_DK)
    gv = gv.reshape(bsz, t_len, GLA_HEADS, GLA_DV)
    log_a = jax.nn.log_sigmoid(gu @ w_gate2 + b_gate) / GLA_TAU
    log_a = jnp.where(vmask, log_a, 0.0).reshape(bsz, t_len, GLA_HEADS, GLA_DK)
    o_gla = _gla(gq, gk, gv, log_a).reshape(bsz, t_len, GLA_V)
    o_gla = _rms_norm_groups(o_gla, norm_w, GLA_HEADS) * jax.nn.silu(gr)
    shp = (bsz, t_len, SB_HEADS, SB_DH)
    o_sb = _stick_breaking(sq.reshape(shp), sk.reshape(shp), sv.reshape(shp), valid)
    o_sb = o_sb.reshape(bsz, t_len, SB_W) * jax.nn.silu(sr)
    mix = jnp.concatenate([o_gla, o_sb], axis=-1)
    return jnp.einsum('bte,ed->btd', mix, w_out)


def _odd_mixer(h, valid, w_in, conv_w, conv_b, dt_bias, a_log, d_skip, norm_w, w_out):
    bsz, t_len, _ = h.shape
    vmask = valid[None, :, None]
    proj = jnp.einsum('btd,de->bte', h, w_in).astype(jnp.float32)
    z, xbc, dt_raw = _split(proj, ODD_SPLITS)
    xbc = jnp.where(vmask, xbc, 0.0)
    xbc = jax.nn.silu(_causal_depthwise_conv(xbc, conv_w, conv_b))
    xs, bs, cs = _split(xbc, (D_INNER, SSM_GROUPS * D_STATE, SSM_GROUPS * D_STATE))
    xs = jnp.where(vmask, xs, 0.0).reshape(bsz, t_len, SSM_GROUPS, SSM_HPG, SSM_HEADDIM)
    bs = bs.reshape(bsz, t_len, SSM_GROUPS, D_STATE)
    cs = cs.reshape(bsz, t_len, SSM_GROUPS, D_STATE)
    dt = jax.nn.softplus(dt_raw + dt_bias).reshape(bsz, t_len, SSM_GROUPS, SSM_HPG)
    y = _ssd(xs, bs, cs, dt, a_log, d_skip)
    y = _rms_norm_groups(y * jax.nn.silu(z), norm_w, SSM_GROUPS)
    return jnp.einsum('bte,ed->btd', y, w_out)


def setup_inputs(seed: int = 0) -> dict:
    key = jax.random.key(seed)
    ks = jax.random.split(key, 17)
    f32 = jnp.float32
    n_even = (DEPTH + 1) // 2
    n_odd = DEPTH // 2

    def nrm(k, shape):
        return jax.random.normal(k, shape, f32)

    dt0 = jnp.exp(jax.random.uniform(ks[10], (n_odd, SSM_HEADS), f32, math.log(1e-3), math.log(1e-1)))
    return {
        'x': nrm(ks[0], (BATCH, SEQ, D_MODEL)),
        'meta': nrm(ks[1], (N_META, D_MODEL)),
        'ev_w_in': nrm(ks[2], (n_even, D_MODEL, EVEN_IN)) * D_MODEL ** -0.5,
        'ev_gla_w_gate2': nrm(ks[3], (n_even, GLA_GATE_RANK, GLA_QK)) * GLA_GATE_RANK ** -0.5,
        'ev_gla_b_gate': 0.1 * nrm(ks[4], (n_even, GLA_QK)),
        'ev_gla_norm_w': 1.0 + 0.02 * nrm(ks[5], (n_even, GLA_V)),
        'ev_w_out': nrm(ks[6], (n_even, EVEN_MIX, D_MODEL)) * (EVEN_MIX ** -0.5 * DEEPNORM_BETA),
        'od_w_in': nrm(ks[7], (n_odd, D_MODEL, ODD_IN)) * D_MODEL ** -0.5,
        'od_conv_w': nrm(ks[8], (n_odd, CONV_K, CONV_DIM)) * CONV_K ** -0.5,
        'od_conv_b': 0.02 * nrm(ks[9], (n_odd, CONV_DIM)),
        'od_dt_bias': dt0 + jnp.log(-jnp.expm1(-dt0)),
        'od_a_log': jnp.log(jax.random.uniform(ks[11], (n_odd, SSM_HEADS), f32, 1.0, 16.0)),
        'od_d_skip': 1.0 + 0.02 * nrm(ks[12], (n_odd, SSM_HEADS)),
        'od_norm_w': 1.0 + 0.02 * nrm(ks[13], (n_odd, D_INNER)),
        'od_w_out': nrm(ks[14], (n_odd, D_INNER, D_MODEL)) * (D_INNER ** -0.5 * DEEPNORM_BETA),
        'ln_g': 1.0 + 0.02 * nrm(ks[15], (DEPTH, D_MODEL)),
        'ln_b': 0.02 * nrm(ks[16], (DEPTH, D_MODEL)),
    }


def reference(x, meta, ev_w_in, ev_gla_w_gate2, ev_gla_b_gate, ev_gla_norm_w, ev_w_out,
              od_w_in, od_conv_w, od_conv_b, od_dt_bias, od_a_log, od_d_skip, od_norm_w,
              od_w_out, ln_g, ln_b):
    bsz = x.shape[0]
    dtype = x.dtype
    pad = jnp.zeros((bsz, PAD_FRONT, D_MODEL), dtype)
    m = jnp.broadcast_to(meta.astype(dtype)[None], (bsz, N_META, D_MODEL))
    h = jnp.concatenate([pad, m, x], axis=1)
    valid = jnp.arange(h.shape[1]) >= PAD_FRONT
    for layer in range(DEPTH):
        j = layer // 2
        if layer % 2 == 0:
            f = _even_mixer(h, valid, ev_w_in[j], ev_gla_w_gate2[j], ev_gla_b_gate[j],
                            ev_gla_norm_w[j], ev_w_out[j])
        else:
            f = _odd_mixer(h, valid, od_w_in[j], od_conv_w[j], od_conv_b[j], od_dt_bias[j],
                           od_a_log[j], od_d_skip[j], od_norm_w[j], od_w_out[j])
        h = _layer_norm(DEEPNORM_ALPHA * h.astype(jnp.float32) + f, ln_g[layer], ln_b[layer]).astype(dtype)
    return h[:, PREFIX:]
```

```python
import contextlib
import numpy as np
import ml_dtypes
import concourse.bass as bass
import concourse.mybir as mybir
from concourse.bass_utils import run_bass_kernel_spmd

F32 = mybir.dt.float32
BF16 = mybir.dt.bfloat16
U8 = mybir.dt.uint8
AF = mybir.ActivationFunctionType
ALU = mybir.AluOpType
NPBF = ml_dtypes.bfloat16

D = 2048
DEPTH = 4
NEG = -30000.0
ALPHA = (2 * DEPTH) ** 0.25
LN_EPS = 1e-5
RMS_EPS = 1e-6
ENGS = ("pe", "act", "dve", "pool", "sp")
ARENA = 206 * 1024


PSUM_PREFIXES = ("bank", "pbig", "za", "arg", "oT", "pX", "pG", "pR", "pS", "pO", "pU", "pA", "pL", "pY", "b1")


class Op:
    __slots__ = ("eng", "fn", "reads", "writes", "chan", "idx", "deps", "signal", "count", "waits")


class Prog:
    def __init__(self, nc):
        self.nc = nc
        self.ops = []
        self.keys = set()

    def op(self, eng, fn, reads=(), writes=(), chan=None):
        o = Op()
        o.eng, o.fn, o.reads, o.writes, o.chan = eng, fn, tuple(reads), tuple(writes), chan
        o.deps, o.signal, o.count, o.waits = [], False, 0, []
        o.idx = len(self.ops)
        self.ops.append(o)
        self.keys.update(o.reads)
        self.keys.update(o.writes)
        return o

    def dma(self, q, out, in_, reads=(), writes=(), chan=None, **kw):
        assert chan is not None
        return self.op(q, lambda e: e.dma_start(out=out, in_=in_, **kw), reads, writes, chan)

    def barrier(self):
        ks = tuple(self.keys)
        for e in ENGS:
            self.op(e, lambda en: en.nop(), reads=(), writes=ks)

    def analyze(self):
        last_w, readers = {}, {}
        ops = self.ops
        for o in ops:
            deps = set()
            for k in o.reads:
                w = last_w.get(k)
                if w is not None:
                    deps.add(w)
                if k.startswith(PSUM_PREFIXES):
                    for r in readers.get(k, ()):
                        if ops[r].eng != o.eng:
                            deps.add(r)
            for k in o.writes:
                w = last_w.get(k)
                if w is not None and not (o.chan is not None and ops[w].chan == o.chan):
                    deps.add(w)
                rl = readers.get(k)
                if rl:
                    deps.update(rl)
            deps.discard(o.idx)
            dl = []
            for d in deps:
                p = ops[d]
                if p.chan is None and o.chan is None and p.eng == "pe" and o.eng == "pe":
                    continue
                dl.append(d)
                p.signal = True
            o.deps = dl
            if o.fn is None:
                continue
            for k in o.reads:
                readers.setdefault(k, []).append(o.idx)
            for k in o.writes:
                last_w[k] = o.idx
                readers[k] = []
        ecount = {e: 0 for e in ENGS}
        ccount = {}
        for o in ops:
            if o.chan is not None:
                ccount[o.chan] = ccount.get(o.chan, 0) + 16
                o.count = ccount[o.chan]
            elif o.signal:
                ecount[o.eng] += 1
                o.count = ecount[o.eng]
        known = {e: {} for e in ENGS}
        for o in ops:
            need = {}
            for d in o.deps:
                p = ops[d]
                s = ("c", p.chan) if p.chan is not None else ("e", p.eng)
                if p.count > need.get(s, 0):
                    need[s] = p.count
            kn = known[o.eng]
            o.waits = []
            for s, v in need.items():
                if kn.get(s, 0) < v:
                    kn[s] = v
                    o.waits.append((s, v))

    def emit(self):
        nc = self.nc
        import os
        if os.environ.get("MK_TRUNC"):
            self.ops = self.ops[:int(os.environ["MK_TRUNC"])]
        self.analyze()
        with contextlib.ExitStack() as st:
            esem = {e: st.enter_context(nc.semaphore("s_" + e)) for e in ENGS}
            csem = {}
            for o in self.ops:
                if o.chan is not None and o.chan not in csem:
                    csem[o.chan] = st.enter_context(nc.semaphore("c_" + str(o.chan)))
            block = st.enter_context(nc.Block())
            per = {e: [o for o in self.ops if o.eng == e] for e in ENGS}

            def run(en):
                def body(e):
                    for o in per[en]:
                        for (kind, nm), v in o.waits:
                            e.wait_ge(esem[nm] if kind == "e" else csem[nm], v)
                        if o.fn is None:
                            continue
                        ins = o.fn(e)
                        if o.chan is not None:
                            ins.then_inc(csem[o.chan], 16)
                        elif o.signal:
                            ins.then_inc(esem[en], 1)
                return body

            block.tensor(run("pe"))
            block.scalar(run("act"))
            block.vector(run("dve"))
            block.gpsimd(run("pool"))
            block.sync(run("sp"))


def _dsize(dt):
    return 4 if dt == F32 else (2 if dt == BF16 else 1)


class Cx:
    def __init__(self, nc):
        self.nc = nc
        self.P = Prog(nc)
        self.arena = nc.alloc_sbuf_tensor("arena", [128, ARENA], U8).ap()
        self.off = 0
        self.pbig = [nc.alloc_psum_tensor("psA", [128, 2048], F32).ap(),
                     nc.alloc_psum_tensor("psB", [128, 2048], F32).ap()]
        self.psn = 0
        self.uid = 0

    def sb(self, cols, dt, parts=128):
        n = cols * _dsize(dt)
        n = (n + 63) // 64 * 64
        assert self.off + n <= ARENA, ("sbuf overflow", self.off, n)
        ap = self.arena[:, self.off:self.off + n].bitcast(dt)[:, 0:cols]
        self.off += n
        return ap if parts == 128 else ap[0:parts, :]

    def bank(self, k):
        return self.pbig[k // 4][:, (k % 4) * 512:(k % 4 + 1) * 512]

    def key(self, s):
        self.uid += 1
        return "%s#%d" % (s, self.uid)

    def dram(self, name, shape, dt, kind="Internal"):
        return self.nc.dram_tensor(name, list(shape), dt, kind=kind).ap()


def make_consts():
    i = np.arange(128)
    same = (i[:, None] // 64) == (i[None, :] // 64)
    le = i[:, None] <= i[None, :]
    c = {}
    c["ident32"] = np.eye(128, dtype=np.float32)
    c["ident16"] = np.eye(128, dtype=np.float32).astype(NPBF)
    c["ones32"] = np.ones((128, 128), np.float32)
    c["gla_tri"] = np.where(same & le, -1.0 / 16.0, 0.0).astype(np.float32)
    c["gla_triu"] = np.where(same & ~le, -1.0 / 16.0, 0.0).astype(np.float32)
    c["gla_mask"] = np.where(same & le, 1.0, 0.0).astype(np.float32)
    c["sb_tri"] = np.where(i[:, None] >= i[None, :], -1.0, 0.0).astype(NPBF)
    c["sb_negones"] = np.full((128, 128), -1.0, np.float32).astype(NPBF)
    c["sb_caus"] = np.where(i[:, None] < i[None, :], 0.0, NEG).astype(NPBF)
    m0 = np.zeros((1, 128), np.float32)
    m0[0, :112] = NEG
    c["sb_m0"] = m0.astype(NPBF)
    c["ones16row"] = np.ones((1, 512), np.float32).astype(NPBF)
    c["ssd_tri"] = np.where(same & le, 1.0, 0.0).astype(np.float32)
    c["ssd_triu"] = np.where(same & ~le, 1.0, 0.0).astype(np.float32)
    c["ssd_mneg"] = np.where(same & le, 0.0, NEG).astype(NPBF)
    c["ssd_mneg4"] = np.tile(np.where(same & le, 0.0, NEG), (1, 4)).astype(NPBF)
    sel = np.zeros((8, 8, 128), np.float32)
    for h in range(8):
        sel[h, h, :] = 1.0
    c["ssd_sel"] = sel.reshape(8, 1024)
    ch = np.zeros((128, 256), np.float32)
    ch[0:64, 0:128] = 1.0
    ch[64:128, 128:256] = 1.0
    c["ssd_ch"] = ch
    return c


CONST_SHAPES = {k: (v.shape, v.dtype) for k, v in make_consts().items()}


def load_consts(cx, names):
    if not hasattr(cx, "cdram"):
        cx.cdram = {}
    out = {}
    for n in names:
        shp, dt = CONST_SHAPES[n]
        bdt = F32 if dt == np.float32 else BF16
        if n not in cx.cdram:
            cx.cdram[n] = cx.dram("c_" + n, shp, bdt, kind="ExternalInput")
        s = cx.sb(shp[1], bdt, parts=shp[0])
        cx.P.dma("sp", s, cx.cdram[n], writes=["c_" + n], chan="ld_c_" + n)
        out[n] = s
    return out


def emit_p3(cx, NT, groups, has_proj, layer_tag):
    P = cx.P
    TT = NT * 128
    h_in = cx.dram("h_in", [TT, D], F32, kind="ExternalInput")
    hT_out = cx.dram("hT_out", [16, 128, TT], BF16, kind="ExternalOutput")
    C = load_consts(cx, ["ident16"])
    hn16 = cx.sb(D, BF16)
    hTs = [cx.sb(16 * 128, BF16) for _ in range(2)]
    if has_proj:
        EC = sum(g[0] for g in groups)
        use_ss = any(g[1] is not None for g in groups)
        h_out = cx.dram("h_out", [TT, D], F32, kind="ExternalOutput")
        mixT = cx.dram("mixT", [EC, 128, TT], BF16, kind="ExternalInput")
        wout = cx.dram("wout", [EC * 128, D], F32, kind="ExternalInput")
        lng = cx.dram("lng", [1, D], F32, kind="ExternalInput")
        lnb = cx.dram("lnb", [1, D], F32, kind="ExternalInput")
        if use_ss:
            ssd = cx.dram("ss", [8, TT], F32, kind="ExternalInput")
        w16 = cx.sb(EC * D, BF16)
        for e in range(EC):
            P.dma("pool", w16[:, e * D:(e + 1) * D], wout[e * 128:(e + 1) * 128, :],
                  writes=["w16"], chan="ld_w16")
        gb = cx.sb(D, F32)
        bb = cx.sb(D, F32)
        P.dma("sp", gb, lng.partition_broadcast(128), writes=["gb"], chan="ld_gb")
        P.dma("sp", bb, lnb.partition_broadcast(128), writes=["bb"], chan="ld_bb")
        mts = [cx.sb(EC * 128, BF16) for _ in range(2)]
        hts = [cx.sb(D, F32) for _ in range(2)]
        racc = cx.sb(D, F32)
        hn = cx.sb(D, F32)
        sst = cx.sb(8, F32)
        ssum = cx.sb(4, F32)
        rsc = cx.sb(4, F32)
        stats = cx.sb(4 * 6, F32)
        mv = cx.sb(2, F32)
        rstd = cx.sb(1, F32)
        nmr = cx.sb(1, F32)
    else:
        hts = [cx.sb(D, F32) for _ in range(2)]

    pset = 0
    for t in range(NT):
        sl = t % 2
        t0 = t * 128
        ht = hts[sl]
        kh = "ht%d" % sl
        P.dma("sp", ht, h_in[t0:t0 + 128, :], writes=[kh], chan="ld_ht%d" % sl)
        if has_proj:
            mt = mts[sl]
            km = "mt%d" % sl
            P.dma("sp", mt.rearrange("p (e t) -> p e t", e=EC),
                  mixT[:, :, t0:t0 + 128].rearrange("e p t -> p e t"),
                  writes=[km], chan="ld_mt%d" % sl)
            if use_ss:
                P.dma("sp", sst, ssd[:, t0:t0 + 128].rearrange("c t -> t c"), writes=["sst"],
                      chan="ld_sst", allow_slow_non_contiguous=True)
                P.op("dve", lambda e: e.tensor_tensor(out=ssum, in0=sst[:, 0:8:2], in1=sst[:, 1:8:2], op=ALU.add),
                     reads=["sst"], writes=["ssum"])
                P.op("act", lambda e: e.activation(out=rsc, in_=ssum, func=AF.Sqrt, bias=RMS_EPS, scale=1.0 / 512.0),
                     reads=["ssum"], writes=["rsc"])
                P.op("dve", lambda e: e.reciprocal(out=rsc, in_=rsc), reads=["rsc"], writes=["rsc"])
            P.op("act", lambda e, ht=ht: e.mul(out=racc, in_=ht, mul=ALPHA), reads=[kh], writes=["racc"])
            ec0 = 0
            for (nch, sc) in groups:
                pb = cx.pbig[pset]
                kp = "pbig%d" % pset
                pset ^= 1
                for n in range(4):
                    for j in range(nch):
                        e_ = ec0 + j
                        P.op("pe", lambda e, pb=pb, mt=mt, e_=e_, n=n, j=j, nch=nch: e.matmul(
                            pb[:, n * 512:(n + 1) * 512], mt[:, e_ * 128:(e_ + 1) * 128],
                            w16[:, e_ * D + n * 512:e_ * D + (n + 1) * 512], start=(j == 0), stop=(j == nch - 1)),
                            reads=[km, "w16"], writes=[kp])
                ec0 += nch
                if sc is None:
                    P.op("dve", lambda e, pb=pb: e.tensor_tensor(out=racc, in0=pb, in1=racc, op=ALU.add),
                         reads=[kp, "racc"], writes=["racc"])
                else:
                    P.op("dve", lambda e, pb=pb, sc=sc: e.scalar_tensor_tensor(
                        out=racc, in0=pb, scalar=rsc[:, sc:sc + 1], in1=racc, op0=ALU.mult, op1=ALU.add),
                        reads=[kp, "racc", "rsc"], writes=["racc"])
            for n in range(4):
                P.op("dve", lambda e, n=n: e.bn_stats(out=stats[:, n * 6:(n + 1) * 6], in_=racc[:, n * 512:(n + 1) * 512]),
                     reads=["racc"], writes=["stats%d" % n])
            P.op("dve", lambda e: e.bn_aggr(out=mv, in_=stats), reads=["stats%d" % n for n in range(4)], writes=["mv"])
            P.op("act", lambda e: e.activation(out=rstd, in_=mv[:, 1:2], func=AF.Sqrt, bias=LN_EPS, scale=1.0),
                 reads=["mv"], writes=["rstd"])
            P.op("dve", lambda e: e.reciprocal(out=rstd, in_=rstd), reads=["rstd"], writes=["rstd"])
            P.op("dve", lambda e: e.tensor_scalar(out=racc, in0=racc, scalar1=mv[:, 0:1], scalar2=rstd,
                                                  op0=ALU.subtract, op1=ALU.mult),
                 reads=["racc", "mv", "rstd"], writes=["racc"])
            P.op("pool", lambda e: e.tensor_tensor(out=racc, in0=racc, in1=gb, op=ALU.mult),
                 reads=["racc", "gb"], writes=["racc"])
            P.op("pool", lambda e: e.tensor_tensor(out=hn, in0=racc, in1=bb, op=ALU.add),
                 reads=["racc", "bb"], writes=["hn"])
            P.dma("pool", h_out[t0:t0 + 128, :], hn, reads=["hn"], chan="st_hn")
            src, ksrc = hn, "hn"
        else:
            src, ksrc = ht, kh
        P.op("act", lambda e, src=src: e.copy(out=hn16, in_=src), reads=[ksrc], writes=["hn16"])
        hTt = hTs[sl]
        kT = "hTs%d" % sl
        for half in range(2):
            pk = 6 + half if not has_proj else None
            if has_proj:
                pb = cx.pbig[pset][:, half * 512:(half + 1) * 512]
                kp = "pbig%d" % pset
            else:
                pb = cx.bank(pk)
                kp = "bank%d" % pk
            pbb = pb.bitcast(BF16)
            for c in range(8):
                cc = half * 8 + c
                P.op("pe", lambda e, pbb=pbb, c=c, cc=cc: e.transpose(
                    pbb[:, c * 128:(c + 1) * 128], hn16[:, cc * 128:(cc + 1) * 128], C["ident16"]),
                    reads=["hn16", "c_ident16"], writes=[kp])
            P.op("dve" if half == 0 else "act",
                 (lambda e, pbb=pbb, hTt=hTt, half=half: e.tensor_copy(out=hTt[:, half * 1024:(half + 1) * 1024], in_=pbb))
                 if half == 0 else
                 (lambda e, pbb=pbb, hTt=hTt, half=half: e.copy(out=hTt[:, half * 1024:(half + 1) * 1024], in_=pbb)),
                 reads=[kp], writes=[kT + "_%d" % half])
        if has_proj:
            pset ^= 1
        P.dma("pool", hT_out[:, :, t0:t0 + 128].rearrange("c p t -> p c t"),
              hTt.rearrange("p (c t) -> p c t", c=16), reads=[kT + "_0", kT + "_1"], chan="st_hT%d" % sl)
    P.op("sp", None, writes=["hn", "hTs0_0", "hTs0_1", "hTs1_0", "hTs1_1"])


def build_p3(NT, groups, has_proj):
    nc = bass.Bass("TRN2", target_bir_lowering=False)
    cx = Cx(nc)
    emit_p3(cx, NT, groups, has_proj, "")
    cx.P.emit()
    return nc


class Stage:
    def __init__(self, cx, name, cols, dt, parts=128, nslots=2):
        self.bufs = [cx.sb(cols, dt, parts) for _ in range(nslots)]
        self.n = 0
        self.name = name

    def next(self):
        i = self.n % len(self.bufs)
        self.n += 1
        return self.bufs[i], "%s_s%d" % (self.name, i), "st_%s%d" % (self.name, i)


def token_tiles(NB):
    tiles = [(0, 128)]
    t = 128
    T = NB * 128
    while t < T:
        n = min(512, T - t)
        tiles.append((t, n))
        t += n
    return tiles


def emit_inproj(cx, NB, hT_d, W_d, ncols, fm, tm, tile_begin=None, tile_end=None):
    P = cx.P
    W16 = cx.sb(16 * ncols, BF16)
    for c in range(16):
        P.dma("pool", W16[:, c * ncols:(c + 1) * ncols], W_d[c * 128:(c + 1) * 128, :], writes=["W16"], chan="ld_W16")
    hts = [cx.sb(16 * 512, BF16) for _ in range(2)]
    evn = [0]

    def nextbank():
        bk = cx.psn % 8
        cx.psn += 1
        return cx.bank(bk), "bank%d" % bk

    for ti, (t0, ntok) in enumerate(token_tiles(NB)):
        sl = ti % 2
        ht = hts[sl]
        kh = "hTt%d" % sl
        P.dma("sp", ht[:, 0:16 * ntok].rearrange("p (c t) -> p c t", c=16),
              hT_d[:, :, t0:t0 + ntok].rearrange("c p t -> p c t"), writes=[kh], chan="ld_hTt%d" % sl)
        if tile_begin:
            tile_begin(ti, t0, ntok)
        for (col0, m, handler) in fm:
            ps, kp = nextbank()
            for c in range(16):
                P.op("pe", lambda e, ps=ps, c=c, col0=col0, m=m, ht=ht, ntok=ntok: e.matmul(
                    ps[0:m, 0:ntok], W16[:, c * ncols + col0:c * ncols + col0 + m], ht[:, c * ntok:(c + 1) * ntok],
                    start=(c == 0), stop=(c == 15)), reads=[kh, "W16"], writes=[kp])
            handler(ps[0:m, 0:ntok], kp, ti, t0, ntok)
        for b in range(ntok // 128):
            for (col0, n, handler) in tm:
                ps, kp = nextbank()
                for c in range(16):
                    P.op("pe", lambda e, ps=ps, c=c, col0=col0, n=n, ht=ht, ntok=ntok, b=b: e.matmul(
                        ps[:, 0:n], ht[:, c * ntok + b * 128:c * ntok + (b + 1) * 128],
                        W16[:, c * ncols + col0:c * ncols + col0 + n],
                        start=(c == 0), stop=(c == 15)), reads=[kh, "W16"], writes=[kp])
                handler(ps[:, 0:n], kp, ti, t0, b)
        if tile_end:
            tile_end(ti, t0, ntok)


def evac_copy(cx, eng, out, in_, reads, writes, scale=None, func=None):
    P = cx.P
    if func is not None:
        P.op("act", lambda e: e.activation(out=out, in_=in_, func=func, scale=(1.0 if scale is None else scale)),
             reads=reads, writes=writes)
    elif eng == "act":
        if scale is None:
            P.op("act", lambda e: e.copy(out=out, in_=in_), reads=reads, writes=writes)
        else:
            P.op("act", lambda e: e.mul(out=out, in_=in_, mul=scale), reads=reads, writes=writes)
    else:
        if scale is None:
            P.op(eng, lambda e: e.tensor_copy(out=out, in_=in_), reads=reads, writes=writes)
        else:
            P.op(eng, lambda e: e.tensor_scalar(out=out, in0=in_, scalar1=scale, scalar2=None, op0=ALU.mult),
                 reads=reads, writes=writes)


E_GQ, E_GK, E_GV, E_GR, E_GU, E_SQ, E_SK, E_SV, E_SR, E_NC = 0, 256, 512, 768, 1024, 1040, 1168, 1296, 1424, 1552


def even_scratch(cx, NB, kind="Internal"):
    T = NB * 128
    d = {}
    d["gqT"] = cx.dram("x_gqT", [2, 128, T], F32, kind)
    d["gkT"] = cx.dram("x_gkT", [2, 128, T], F32, kind)
    d["sgrT"] = cx.dram("x_sgrT", [2, 128, T], F32, kind)
    d["guT"] = cx.dram("x_guT", [16, T], F32, kind)
    d["gkt"] = cx.dram("x_gkt", [128, NB, 256], F32, kind)
    d["gvt"] = cx.dram("x_gvt", [128, NB, 256], BF16, kind)
    d["qT"] = cx.dram("x_qT", [128, T], BF16, kind)
    d["kT"] = cx.dram("x_kT", [128, T], BF16, kind)
    d["srT"] = cx.dram("x_srT", [128, T], F32, kind)
    d["V"] = cx.dram("x_V", [128, NB, 128], BF16, kind)
    return d


def emit_even_inproj(cx, NB, hT_d, W_d, S):
    P = cx.P
    st = {
        "gq": Stage(cx, "gq", 1024, F32), "gk": Stage(cx, "gk", 1024, F32), "gr": Stage(cx, "gr", 1024, F32),
        "sq": Stage(cx, "sq", 512, BF16), "sk": Stage(cx, "sk", 512, BF16), "sr": Stage(cx, "sr", 512, F32),
        "gu": Stage(cx, "gu", 512, F32, parts=16),
        "gkt": Stage(cx, "gkt", 4 * 256, F32), "gvt": Stage(cx, "gvt", 4 * 256, BF16), "V": Stage(cx, "V", 4 * 128, BF16),
    }
    cur = {}
    tog = [0]

    def eng():
        tog[0] ^= 1
        return "act" if tog[0] else "dve"

    def tile_begin(ti, t0, ntok):
        for k, s in st.items():
            cur[k] = s.next()

    def fm_pair(name, j, scale=None, func=None, maskpad=False):
        def h(ps, kp, ti, t0, ntok):
            buf, kb, ch = cur[name]
            dst = buf[:, j * ntok:(j + 1) * ntok]
            evac_copy(cx, eng(), dst, ps, [kp], [kb + "_%d" % j], scale=scale, func=func)
            if maskpad and ti == 0:
                P.op("pool", lambda e: e.memset(buf[:, j * ntok:j * ntok + 112], 0.0), writes=[kb + "_%d" % j])
        return h

    def fm_single(name, scale=None, func=None, parts=128):
        def h(ps, kp, ti, t0, ntok):
            buf, kb, ch = cur[name]
            evac_copy(cx, eng(), buf[0:parts, 0:ntok], ps, [kp], [kb], scale=scale, func=func)
        return h

    def tm_A(ps, kp, ti, t0, b):
        buf, kb, ch = cur["gkt"]
        evac_copy(cx, eng(), buf[:, b * 256:(b + 1) * 256], ps[:, 0:256], [kp], [kb + "_%d" % b])
        if ti == 0:
            P.op("pool", lambda e: e.memset(buf[0:112, 0:256], 0.0), writes=[kb + "_0"])
        buf2, kb2, ch2 = cur["gvt"]
        evac_copy(cx, eng(), buf2[:, b * 256:(b + 1) * 256], ps[:, 256:512], [kp], [kb2 + "_%d" % b])

    def tm_B(ps, kp, ti, t0, b):
        buf, kb, ch = cur["V"]
        evac_copy(cx, eng(), buf[:, b * 128:(b + 1) * 128], ps, [kp], [kb + "_%d" % b])

    def tile_end(ti, t0, ntok):
        nb = ntok // 128
        b0 = t0 // 128
        for name, dd in (("gq", S["gqT"]), ("gk", S["gkT"]), ("gr", S["sgrT"])):
            buf, kb, ch = cur[name]
            P.dma("pool", dd[:, :, t0:t0 + ntok].rearrange("j p t -> p j t"),
                  buf[:, 0:2 * ntok].rearrange("p (j t) -> p j t", j=2), reads=[kb + "_0", kb + "_1"], chan=ch)
        for name, dd in (("sq", S["qT"]), ("sk", S["kT"]), ("sr", S["srT"])):
            buf, kb, ch = cur[name]
            P.dma("pool", dd[:, t0:t0 + ntok], buf[:, 0:ntok], reads=[kb], chan=ch)
        buf, kb, ch = cur["gu"]
        P.dma("pool", S["guT"][:, t0:t0 + ntok], buf[0:16, 0:ntok], reads=[kb], chan=ch)
        for name, dd, w in (("gkt", S["gkt"], 256), ("gvt", S["gvt"], 256), ("V", S["V"], 128)):
            buf, kb, ch = cur[name]
            P.dma("pool", dd[:, b0:b0 + nb, :], buf[:, 0:nb * w].rearrange("p (b w) -> p b w", b=nb),
                  reads=[kb + "_%d" % b for b in range(nb)], chan=ch)

    fm = [
        (E_GQ, 128, fm_pair("gq", 0, scale=1.0 / 16.0)), (E_GQ + 128, 128, fm_pair("gq", 1, scale=1.0 / 16.0)),
        (E_GK, 128, fm_pair("gk", 0, maskpad=True)), (E_GK + 128, 128, fm_pair("gk", 1, maskpad=True)),
        (E_GR, 128, fm_pair("gr", 0, func=AF.Silu)), (E_GR + 128, 128, fm_pair("gr", 1, func=AF.Silu)),
        (E_SQ, 128, fm_single("sq", scale=128.0 ** -0.5)),
        (E_SK, 128, fm_single("sk")),
        (E_SR, 128, fm_single("sr", func=AF.Silu)),
        (E_GU, 16, fm_single("gu", parts=16)),
    ]
    tm = [(E_GK, 512, tm_A), (E_SV, 128, tm_B)]
    emit_inproj(cx, NB, hT_d, W_d, E_NC, fm, tm, tile_begin, tile_end)
    allk = []
    for k, s in st.items():
        for i in range(len(s.bufs)):
            base = "%s_s%d" % (k, i)
            allk += [base] + [base + "_%d" % j for j in range(4)]
    P.op("sp", None, writes=allk)
    return allk


def sb_qtiles(NB):
    tiles = [(0, 1)]
    b = 1
    while b < NB:
        n = min(4, NB - b)
        tiles.append((b, n))
        b += n
    return tiles


def emit_sb(cx, NB, S, mix_sb_d):
    P = cx.P
    T = NB * 128
    C = load_consts(cx, ["sb_tri", "sb_negones", "sb_caus", "sb_m0", "ones16row", "ident16"])
    kT = cx.sb(T, BF16)
    V = cx.sb(T, BF16)
    nld = max(1, T // 4096)
    for i in range(nld):
        a, b = i * T // nld, (i + 1) * T // nld
        P.dma("sp", kT[:, a:b], S["kT"][:, a:b], writes=["kT"], chan="ld_kT")
    P.dma("sp", V.rearrange("p (n d) -> p n d", d=128), S["V"], writes=["V"], chan="ld_V")
    qts = [cx.sb(512, BF16) for _ in range(3)]
    srts = [cx.sb(512, F32) for _ in range(3)]
    ebuf = [cx.sb(512, F32) for _ in range(2)]
    spbuf = [cx.sb(512, BF16) for _ in range(3)]
    wbuf = [cx.sb(512, BF16) for _ in range(2)]
    S32 = cx.sb(512, F32)
    S16 = [cx.sb(512, BF16) for _ in range(2)]
    ost = [cx.sb(512, BF16) for _ in range(2)]
    tiles = sb_qtiles(NB)
    steps = []
    for ti, (b0, nb) in enumerate(tiles):
        for n in range(b0 + nb - 1, -1, -1):
            steps.append((ti, n))

    def load_q(ti):
        b0, nb = tiles[ti]
        W = nb * 128
        sl = ti % 3
        P.dma("sp", qts[sl][:, 0:W], S["qT"][:, b0 * 128:b0 * 128 + W], writes=["qt%d" % sl], chan="ld_qt%d" % sl)
        P.dma("sp", srts[sl][:, 0:W], S["srT"][:, b0 * 128:b0 * 128 + W], writes=["srt%d" % sl], chan="ld_srt%d" % sl)

    def geom(i):
        ti, n = steps[i]
        b0, nb = tiles[ti]
        W = nb * 128
        diag = n >= b0
        coff = (n - b0) * 128 if diag else 0
        first = (n == b0 + nb - 1)
        return ti, n, b0, nb, W, diag, coff, first

    def group(ps, kp, terms):
        for j, (c0, c1, lt, rh, rd) in enumerate(terms):
            P.op("pe", lambda e, j=j, c0=c0, c1=c1, lt=lt, rh=rh: e.matmul(
                ps[:, c0:c1], lt, rh, start=(j == 0), stop=(j == len(terms) - 1), skip_group_check=True),
                reads=rd, writes=[kp])

    def masks(n, diag, coff, W):
        lst = []
        if diag:
            lst.append((coff, coff + 128, C["ident16"], C["sb_caus"], ["c_ident16", "c_sb_caus"]))
        if n == 0:
            lst.append((coff, W, C["sb_m0"], C["ones16row"][:, 0:W - coff], ["c_sb_m0", "c_ones16row"]))
        return lst

    def stageA(i):
        ti, n, b0, nb, W, diag, coff, first = geom(i)
        if first:
            if ti + 1 < len(tiles):
                load_q(ti + 1)
        sl = ti % 3
        qt = qts[sl]
        za = cx.bank(i % 2)
        kz = "za%d" % (i % 2)
        terms = masks(n, diag, coff, W)
        terms.append((coff, W, kT[:, n * 128:(n + 1) * 128], qt[:, coff:W], ["kT", "qt%d" % sl]))
        group(za, kz, terms)
        eb = ebuf[i % 2]
        ke = "e%d" % (i % 2)
        P.op("act", lambda e: e.activation(out=eb[:, coff:W], in_=za[:, coff:W], func=AF.Exp), reads=[kz], writes=[ke])
        sp = spbuf[i % 3]
        ks = "sp%d" % (i % 3)
        P.op("act", lambda e: e.activation(out=sp[:, coff:W], in_=eb[:, coff:W], func=AF.Ln, bias=1.0, scale=1.0),
             reads=[ke], writes=[ks])

    def stageD(i):
        ti, n, b0, nb, W, diag, coff, first = geom(i)
        sp = spbuf[i % 3]
        ks = "sp%d" % (i % 3)
        if first:
            P.op("pool", lambda e: e.memset(S32[:, 0:W], 0.0), writes=["S32"])
        if n == 0:
            return
        P.op("dve", lambda e: e.tensor_tensor(out=S32[:, coff:W], in0=S32[:, coff:W], in1=sp[:, coff:W], op=ALU.add),
             reads=["S32", ks], writes=["S32"])
        s16 = S16[i % 2]
        P.op("dve", lambda e: e.tensor_copy(out=s16[:, 0:W], in_=S32[:, 0:W]), reads=["S32"], writes=["S16_%d" % (i % 2)])

    def stageB(i):
        ti, n, b0, nb, W, diag, coff, first = geom(i)
        sl = ti % 3
        qt = qts[sl]
        sp = spbuf[i % 3]
        ks = "sp%d" % (i % 3)
        ar = cx.bank(2 + i % 2)
        ka = "arg%d" % (i % 2)
        terms = masks(n, diag, coff, W)
        terms.append((coff, W, C["sb_tri"], sp[:, coff:W], [ks, "c_sb_tri"]))
        if not first:
            s16 = S16[(i - 1) % 2]
            terms.append((coff, W, C["sb_negones"], s16[:, coff:W], ["S16_%d" % ((i - 1) % 2), "c_sb_negones"]))
        terms.append((coff, W, kT[:, n * 128:(n + 1) * 128], qt[:, coff:W], ["kT", "qt%d" % sl]))
        group(ar, ka, terms)
        wb = wbuf[i % 2]
        kw = "w%d" % (i % 2)
        P.op("act", lambda e: e.activation(out=wb[:, coff:W], in_=ar[:, coff:W], func=AF.Exp), reads=[ka], writes=[kw])
        oT = cx.bank(4 + ti % 2)
        ko = "oT%d" % (ti % 2)
        Vn = V[:, n * 128:(n + 1) * 128]
        last = (n == 0)
        P.op("pe", lambda e: e.matmul(oT[:, coff:W], Vn, wb[:, coff:W], start=first, stop=last, skip_group_check=True),
             reads=["V", kw], writes=[ko])
        if last:
            o16 = ost[ti % 2]
            kos = "ost%d" % (ti % 2)
            srt = srts[sl]
            P.op("dve", lambda e: e.tensor_tensor(out=o16[:, 0:W], in0=oT[:, 0:W], in1=srt[:, 0:W], op=ALU.mult),
                 reads=[ko, "srt%d" % sl], writes=[kos])
            P.dma("pool", mix_sb_d[:, b0 * 128:b0 * 128 + W], o16[:, 0:W], reads=[kos], chan="st_ost%d" % (ti % 2))

    load_q(0)
    stageA(0)
    for i in range(len(steps)):
        if i + 1 < len(steps):
            stageA(i + 1)
        stageD(i)
        stageB(i)
    P.op("sp", None, writes=["ost0", "ost1"])


def emit_gla(cx, NB, S, waug_d, normw_d, mix_gla_d, ss_d):
    P = cx.P
    C = load_consts(cx, ["gla_tri", "gla_triu", "gla_mask", "ones32"])
    waug = cx.sb(256, F32, parts=17)
    P.dma("sp", waug, waug_d, writes=["waug"], chan="ld_waug")
    normw = cx.sb(2, F32)
    P.dma("sp", normw, normw_d, writes=["normw"], chan="ld_normw")
    tl = {k: [cx.sb(n, dt, parts=pp) for _ in range(2)] for k, n, dt, pp in (
        ("gq", 1024, F32, 128), ("gk", 1024, F32, 128), ("sgr", 1024, F32, 128), ("aug", 512, F32, 17),
        ("gkt", 1024, F32, 128), ("gvt", 1024, BF16, 128))}
    for i in range(2):
        P.op("pool", lambda e, i=i: e.memset(tl["aug"][i], 1.0), writes=["aug%d" % i])
    blk = {k: [cx.sb(n, dt) for _ in range(2)] for k, n, dt in (
        ("ex", 256, F32), ("sp", 256, F32), ("Eq", 256, F32), ("Einv", 256, F32), ("Erev", 256, F32),
        ("qd", 256, BF16), ("ki", 256, BF16), ("kend", 256, BF16), ("scm", 128, BF16), ("osq", 256, F32))}
    S32 = cx.sb(512, F32)
    S16 = [cx.sb(512, BF16) for _ in range(2)]
    P.op("pool", lambda e: e.memset(S32, 0.0), writes=["S32"])
    P.op("pool", lambda e: e.memset(S16[1], 0.0), writes=["S16_1"])
    ost = [cx.sb(1024, BF16) for _ in range(2)]
    ssr = [cx.sb(512, F32, parts=1) for _ in range(2)]
    pX, pG, pR, pS, pO, pU, pSS = [cx.bank(i) for i in range(7)]
    tiles = token_tiles(NB)
    blocks = []
    for ti, (t0, ntok) in enumerate(tiles):
        for b in range(ntok // 128):
            blocks.append((ti, t0, ntok, b))

    def load_tile(ti):
        t0, ntok = tiles[ti]
        sl = ti % 2
        nb = ntok // 128
        b0 = t0 // 128
        for name, src in (("gq", S["gqT"]), ("gk", S["gkT"]), ("sgr", S["sgrT"])):
            P.dma("sp", tl[name][sl][:, 0:2 * ntok].rearrange("p (j t) -> p j t", j=2),
                  src[:, :, t0:t0 + ntok].rearrange("j p t -> p j t"), writes=["%s%d" % (name, sl)],
                  chan="ld_%s%d" % (name, sl))
        P.dma("sp", tl["aug"][sl][0:16, 0:ntok], S["guT"][:, t0:t0 + ntok], writes=["aug%d" % sl], chan="ld_aug%d" % sl)
        P.dma("sp", tl["gkt"][sl][:, 0:nb * 256].rearrange("p (b w) -> p b w", b=nb), S["gkt"][:, b0:b0 + nb, :],
              writes=["gkt%d" % sl], chan="ld_gkt%d" % sl)
        P.dma("sp", tl["gvt"][sl][:, 0:nb * 256].rearrange("p (b w) -> p b w", b=nb), S["gvt"][:, b0:b0 + nb, :],
              writes=["gvt%d" % sl], chan="ld_gvt%d" % sl)

    def prep(bi):
        ti, t0, ntok, b = blocks[bi]
        sl = ti % 2
        pb = bi % 2
        aug = tl["aug"][sl]
        ex, sp, Eq, Einv, Erev = (blk[k][pb] for k in ("ex", "sp", "Eq", "Einv", "Erev"))
        qd, ki, kend, scm = (blk[k][pb] for k in ("qd", "ki", "kend", "scm"))
        K = lambda k: "%s_%d" % (k, pb)
        P.op("pe", lambda e: e.matmul(pX[:, 0:256], aug[0:17, b * 128:(b + 1) * 128], waug[0:17, 0:256], start=True, stop=True),
             reads=["aug%d" % sl, "waug"], writes=["pX"])
        P.op("act", lambda e: e.activation(out=ex, in_=pX[:, 0:256], func=AF.Exp, scale=-1.0), reads=["pX"], writes=[K("ex")])
        P.op("act", lambda e: e.activation(out=sp, in_=ex, func=AF.Ln, bias=1.0, scale=1.0), reads=[K("ex")], writes=[K("sp")])
        if t0 == 0 and b == 0:
            P.op("pool", lambda e: e.memset(sp[0:112, :], 0.0), writes=[K("sp")])
        for j in range(2):
            P.op("pe", lambda e, j=j: e.matmul(pG[:, j * 128:(j + 1) * 128], sp[:, j * 128:(j + 1) * 128], C["gla_tri"],
                                               start=True, stop=True), reads=[K("sp"), "c_gla_tri"], writes=["pG"])
        P.op("pe", lambda e: e.matmul(pR[:, 0:256], C["gla_triu"], sp, start=True, stop=True),
             reads=[K("sp"), "c_gla_triu"], writes=["pR"])
        P.op("act", lambda e: e.activation(out=Eq, in_=pG[:, 0:256], func=AF.Exp), reads=["pG"], writes=[K("Eq")])
        P.op("act", lambda e: e.activation(out=Einv, in_=pG[:, 0:256], func=AF.Exp, scale=-1.0), reads=["pG"], writes=[K("Einv")])
        P.op("act", lambda e: e.activation(out=Erev, in_=pR[:, 0:256], func=AF.Exp), reads=["pR"], writes=[K("Erev")])
        gq3 = tl["gq"][sl][:, 0:2 * ntok].rearrange("p (j t) -> p j t", j=2)[:, :, b * 128:(b + 1) * 128]
        gk3 = tl["gk"][sl][:, 0:2 * ntok].rearrange("p (j t) -> p j t", j=2)[:, :, b * 128:(b + 1) * 128]
        r3 = lambda a: a.rearrange("p (j t) -> p j t", j=2)
        P.op("dve", lambda e: e.tensor_tensor(out=r3(qd), in0=gq3, in1=r3(Eq), op=ALU.mult),
             reads=["gq%d" % sl, K("Eq")], writes=[K("qd")])
        P.op("dve", lambda e: e.tensor_tensor(out=r3(ki), in0=gk3, in1=r3(Einv), op=ALU.mult),
             reads=["gk%d" % sl, K("Einv")], writes=[K("ki")])
        P.op("dve", lambda e: e.tensor_tensor(out=kend, in0=tl["gkt"][sl][:, b * 256:(b + 1) * 256], in1=Erev, op=ALU.mult),
             reads=["gkt%d" % sl, K("Erev")], writes=[K("kend")])
        for j in range(2):
            P.op("pe", lambda e, j=j: e.matmul(pS[:, 0:128], ki[:, j * 128:(j + 1) * 128], qd[:, j * 128:(j + 1) * 128],
                                               start=(j == 0), stop=(j == 1)), reads=[K("ki"), K("qd")], writes=["pS"])
        P.op("dve", lambda e: e.tensor_tensor(out=scm, in0=pS[:, 0:128], in1=C["gla_mask"], op=ALU.mult),
             reads=["pS", "c_gla_mask"], writes=[K("scm")])

    def scan(bi):
        ti, t0, ntok, b = blocks[bi]
        sl = ti % 2
        pb = bi % 2
        Eq, qd, kend, scm, osq = (blk[k][pb] for k in ("Eq", "qd", "kend", "scm", "osq"))
        K = lambda k: "%s_%d" % (k, pb)
        if b == 0 and ti >= 1 and ti + 1 < len(tiles):
            load_tile(ti + 1)
        gv = tl["gvt"][sl][:, b * 256:(b + 1) * 256]
        kgv = "gvt%d" % sl
        for c in range(2):
            ci = 2 * bi + c
            cs = c * 64
            s16 = S16[(ci + 1) % 2]
            ks16 = "S16_%d" % ((ci + 1) % 2)
            s16n = S16[ci % 2]
            ks16n = "S16_%d" % (ci % 2)
            for jv in range(2):
                o = pO[:, jv * 128 + cs:jv * 128 + cs + 64]
                P.op("pe", lambda e, o=o, jv=jv, cs=cs: e.matmul(o, gv[:, jv * 128:(jv + 1) * 128], scm[:, cs:cs + 64],
                                                               start=True, stop=False), reads=[kgv, K("scm")], writes=["pO"])
                for j in range(2):
                    P.op("pe", lambda e, o=o, jv=jv, j=j, cs=cs, s16=s16: e.matmul(
                        o, s16[:, j * 256 + jv * 128:j * 256 + (jv + 1) * 128], qd[:, j * 128 + cs:j * 128 + cs + 64],
                        start=False, stop=(j == 1)), reads=[ks16, K("qd")], writes=["pO"])
            for j in range(2):
                P.op("pe", lambda e, j=j, cs=cs: e.matmul(pU[:, j * 256:(j + 1) * 256], kend[cs:cs + 64, j * 128:(j + 1) * 128],
                                                         gv[cs:cs + 64, 0:256], start=True, stop=True),
                     reads=[K("kend"), kgv], writes=["pU"])
            for j in range(2):
                P.op("dve", lambda e, j=j, cs=cs: e.scalar_tensor_tensor(
                    out=S32[:, j * 256:(j + 1) * 256], in0=S32[:, j * 256:(j + 1) * 256],
                    scalar=Eq[:, j * 128 + cs + 63:j * 128 + cs + 64], in1=pU[:, j * 256:(j + 1) * 256],
                    op0=ALU.mult, op1=ALU.add), reads=["S32", K("Eq"), "pU"], writes=["S32"])
            P.op("act", lambda e, s16n=s16n: e.copy(out=s16n, in_=S32), reads=["S32"], writes=[ks16n])
        sgr = tl["sgr"][sl]
        o16, kos = ost[ti % 2], "gost%d" % (ti % 2)
        P.op("act", lambda e: e.activation(out=osq, in_=pO[:, 0:256], func=AF.Square), reads=["pO"], writes=[K("osq")])
        for jv in range(2):
            P.op("pe", lambda e, jv=jv: e.matmul(pSS[0:1, 0:128], C["ones32"][:, 0:1], osq[:, jv * 128:(jv + 1) * 128],
                                                 start=(jv == 0), stop=(jv == 1)), reads=[K("osq"), "c_ones32"], writes=["pSS"])
        P.op("dve", lambda e: e.tensor_copy(out=ssr[ti % 2][0:1, b * 128:(b + 1) * 128], in_=pSS[0:1, 0:128]),
             reads=["pSS"], writes=["ssr%d_%d" % (ti % 2, b)])
        for jv in range(2):
            P.op("dve", lambda e, jv=jv: e.scalar_tensor_tensor(
                out=o16[:, jv * ntok + b * 128:jv * ntok + (b + 1) * 128], in0=pO[:, jv * 128:(jv + 1) * 128],
                scalar=normw[:, jv:jv + 1], in1=sgr[:, jv * ntok + b * 128:jv * ntok + (b + 1) * 128],
                op0=ALU.mult, op1=ALU.mult), reads=["pO", "normw", "sgr%d" % sl], writes=[kos + "_%d" % b])
        if b == ntok // 128 - 1:
            nb = ntok // 128
            P.dma("pool", mix_gla_d[:, :, t0:t0 + ntok].rearrange("j p t -> p j t"),
                  o16[:, 0:2 * ntok].rearrange("p (j t) -> p j t", j=2), reads=[kos + "_%d" % x for x in range(nb)],
                  chan="st_gost%d" % (ti % 2))
            P.dma("pool", ss_d[0:1, t0:t0 + ntok], ssr[ti % 2][0:1, 0:ntok],
                  reads=["ssr%d_%d" % (ti % 2, x) for x in range(nb)], chan="st_ssr%d" % (ti % 2))

    load_tile(0)
    if len(tiles) > 1:
        load_tile(1)
    prep(0)
    for bi in range(len(blocks)):
        if bi + 1 < len(blocks):
            prep(bi + 1)
        scan(bi)
    P.op("sp", None, writes=["gost0_%d" % x for x in range(4)] + ["gost1_%d" % x for x in range(4)]
         + ["ssr%d_%d" % (i, x) for i in range(2) for x in range(4)])


O_Z, O_X, O_B, O_C, O_DT, O_NC = 0, 512, 1024, 1152, 1280, 1288


def odd_scratch(cx, NB, kind="Internal"):
    T = NB * 128
    return {"xbcT": cx.dram("y_xbcT", [6, 128, T], F32, kind),
            "szt": cx.dram("y_szt", [128, NB, 512], F32, kind),
            "dtr": cx.dram("y_dtr", [128, NB, 8], F32, kind)}


def emit_odd_inproj(cx, NB, hT_d, W_d, S):
    P = cx.P
    st = {"xbc": Stage(cx, "xbc", 6 * 512, F32), "szt": Stage(cx, "szt", 4 * 512, F32), "dtr": Stage(cx, "dtr", 32, F32)}
    cur = {}
    tog = [0]

    def eng():
        tog[0] ^= 1
        return "act" if tog[0] else "dve"

    def tile_begin(ti, t0, ntok):
        for k, s in st.items():
            cur[k] = s.next()

    def fm_x(cb):
        def h(ps, kp, ti, t0, ntok):
            buf, kb, ch = cur["xbc"]
            evac_copy(cx, eng(), buf[:, cb * ntok:(cb + 1) * ntok], ps, [kp], [kb + "_%d" % cb])
            if ti == 0:
                P.op("pool", lambda e: e.memset(buf[:, cb * ntok:cb * ntok + 112], 0.0), writes=[kb + "_%d" % cb])
        return h

    def tm_z(ps, kp, ti, t0, b):
        buf, kb, ch = cur["szt"]
        evac_copy(cx, "act", buf[:, b * 512:(b + 1) * 512], ps, [kp], [kb + "_%d" % b], func=AF.Silu)

    def tm_dt(ps, kp, ti, t0, b):
        buf, kb, ch = cur["dtr"]
        evac_copy(cx, "dve", buf[:, b * 8:(b + 1) * 8], ps, [kp], [kb + "_%d" % b])

    def tile_end(ti, t0, ntok):
        nb = ntok // 128
        b0 = t0 // 128
        buf, kb, ch = cur["xbc"]
        P.dma("pool", S["xbcT"][:, :, t0:t0 + ntok].rearrange("j p t -> p j t"),
              buf[:, 0:6 * ntok].rearrange("p (j t) -> p j t", j=6), reads=[kb + "_%d" % j for j in range(6)], chan=ch)
        for name, w in (("szt", 512), ("dtr", 8)):
            buf, kb, ch = cur[name]
            P.dma("pool", S[name][:, b0:b0 + nb, :], buf[:, 0:nb * w].rearrange("p (b w) -> p b w", b=nb),
                  reads=[kb + "_%d" % b for b in range(nb)], chan=ch)

    fm = [(O_X + 128 * cb, 128, fm_x(cb)) for cb in range(6)]
    tm = [(O_Z, 512, tm_z), (O_DT, 8, tm_dt)]
    emit_inproj(cx, NB, hT_d, W_d, O_NC, fm, tm, tile_begin, tile_end)


def emit_ssd(cx, NB, S, prm, mix_d):
    P = cx.P
    C = load_consts(cx, ["ident32", "ident16", "ssd_ch", "ssd_tri", "ssd_triu", "ssd_mneg4", "ones32"])
    convw = cx.sb(24, F32)
    convb = cx.sb(6, F32)
    dtb = cx.sb(8, F32)
    aneg = cx.sb(8, F32)
    dsk = cx.sb(8, F32)
    nwb = cx.sb(512, F32)
    P.dma("sp", convw, prm["convw"], writes=["convw"], chan="ld_convw")
    P.dma("sp", convb, prm["convb"], writes=["convb"], chan="ld_convb")
    P.dma("sp", dtb, prm["dtb"].partition_broadcast(128), writes=["dtb"], chan="ld_dtb")
    P.dma("sp", aneg, prm["alog"].partition_broadcast(128), writes=["aneg"], chan="ld_aneg")
    P.dma("sp", dsk, prm["dskip"].partition_broadcast(128), writes=["dsk"], chan="ld_dsk")
    P.dma("sp", nwb, prm["normw"].partition_broadcast(128), writes=["nwb"], chan="ld_nwb")
    P.op("act", lambda e: e.activation(out=aneg, in_=aneg, func=AF.Exp), reads=["aneg"], writes=["aneg"])
    P.op("dve", lambda e: e.tensor_scalar(out=aneg, in0=aneg, scalar1=-1.0, scalar2=None, op0=ALU.mult),
         reads=["aneg"], writes=["aneg"])
    tiles = token_tiles(NB)
    ubuf = [cx.sb(6 * 515, F32) for _ in range(2)]
    sztb = [cx.sb(4 * 512, F32) for _ in range(3)]
    dtrb = [cx.sb(32, F32) for _ in range(3)]
    xsT = [cx.sb(4 * 512, F32) for _ in range(2)]
    BT16 = [cx.sb(512, BF16) for _ in range(2)]
    CT16 = [cx.sb(512, BF16) for _ in range(2)]
    cacc = [cx.sb(512, F32) for _ in range(2)]
    blk = {k: [cx.sb(n, dt) for _ in range(2)] for k, n, dt in (
        ("xtok", 512, F32), ("Btok", 128, BF16), ("dtx", 8, F32), ("dt", 8, F32), ("a", 8, F32), ("eacum", 8, F32),
        ("erev", 8, F32), ("dec", 16, F32), ("nacum", 8, F32), ("de", 8, F32), ("xdt", 512, BF16), ("xdd", 512, BF16),
        ("xD", 512, F32), ("R", 1024, F32), ("cb", 128, F32), ("Eall", 1024, F32), ("M16", 1024, BF16), ("ysb", 512, F32))}
    acT8 = [cx.sb(128, F32, parts=8) for _ in range(2)]
    hs32 = cx.sb(512, F32)
    hs16 = [cx.sb(512, BF16) for _ in range(2)]
    yz = cx.sb(512, F32)
    ysq = cx.sb(512, F32)
    ssq = cx.sb(1, F32)
    rstd = cx.sb(1, F32)
    yn16 = cx.sb(512, BF16)
    ost = [cx.sb(4 * 512, BF16) for _ in range(2)]
    P.op("pool", lambda e: e.memset(hs32, 0.0), writes=["hs32"])
    P.op("pool", lambda e: e.memset(hs16[1], 0.0), writes=["hs16_1"])
    pXT = cx.bank(0)
    b1 = cx.bank(1)
    pCB = b1[:, 0:128]
    pBt = b1[:, 128:192].bitcast(BF16)
    pT = b1[:, 256:512].bitcast(BF16)
    pA = cx.bank(2)
    pL = [cx.bank(3), cx.bank(4)]
    pY, pYo, pU = cx.bank(5), cx.bank(6), cx.bank(7)
    blocks = []
    for ti, (t0, ntok) in enumerate(tiles):
        for b in range(ntok // 128):
            blocks.append((ti, t0, ntok, b))
    h3 = lambda a: a.rearrange("p (h q) -> p h q", h=8)
    bc = lambda a8, pp: a8.unsqueeze(2).broadcast_to([pp, 8, 64])

    def load_tile(ti):
        t0, ntok = tiles[ti]
        sl2, sl3 = ti % 2, ti % 3
        nb = ntok // 128
        b0 = t0 // 128
        u3 = ubuf[sl2].rearrange("p (j t) -> p j t", j=6)
        if t0 == 0:
            P.op("pool", lambda e: e.memset(ubuf[sl2], 0.0), writes=["u%d" % sl2])
            P.dma("sp", u3[:, :, 3:3 + ntok], S["xbcT"][:, :, 0:ntok].rearrange("j p t -> p j t"),
                  writes=["u%d" % sl2], chan="ld_u%d" % sl2)
        else:
            P.dma("sp", u3[:, :, 0:3 + ntok], S["xbcT"][:, :, t0 - 3:t0 + ntok].rearrange("j p t -> p j t"),
                  writes=["u%d" % sl2], chan="ld_u%d" % sl2)
        P.dma("sp", sztb[sl3][:, 0:nb * 512].rearrange("p (b w) -> p b w", b=nb), S["szt"][:, b0:b0 + nb, :],
              writes=["szt%d" % sl3], chan="ld_szt%d" % sl3)
        P.dma("sp", dtrb[sl3][:, 0:nb * 8].rearrange("p (b w) -> p b w", b=nb), S["dtr"][:, b0:b0 + nb, :],
              writes=["dtr%d" % sl3], chan="ld_dtr%d" % sl3)

    def conv(ti):
        t0, ntok = tiles[ti]
        sl = ti % 2
        u3 = ubuf[sl].rearrange("p (j t) -> p j t", j=6)
        ku = "u%d" % sl
        for cb in range(6):
            acc = cacc[cb % 2]
            ka = "cacc%d" % (cb % 2)
            eng = "dve" if cb % 2 == 0 else "pool"
            P.op("dve", lambda e, cb=cb, acc=acc: e.tensor_scalar(
                out=acc[:, 0:ntok], in0=u3[:, cb, 3:3 + ntok], scalar1=convw[:, cb * 4 + 3:cb * 4 + 4], scalar2=None,
                op0=ALU.mult), reads=[ku, "convw"], writes=[ka])
            for i in (2, 1, 0):
                P.op("dve", lambda e, cb=cb, acc=acc, i=i: e.scalar_tensor_tensor(
                    out=acc[:, 0:ntok], in0=u3[:, cb, i:i + ntok], scalar=convw[:, cb * 4 + i:cb * 4 + i + 1],
                    in1=acc[:, 0:ntok], op0=ALU.mult, op1=ALU.add), reads=[ku, "convw", ka], writes=[ka])
            if cb < 4:
                dst, kd = xsT[sl][:, cb * ntok:(cb + 1) * ntok], "xsT%d_%d" % (sl, cb)
            elif cb == 4:
                dst, kd = BT16[sl][:, 0:ntok], "BT16_%d" % sl
            else:
                dst, kd = CT16[sl][:, 0:ntok], "CT16_%d" % sl
            P.op("act", lambda e, cb=cb, acc=acc, dst=dst: e.activation(
                out=dst, in_=acc[:, 0:ntok], func=AF.Silu, bias=convb[:, cb:cb + 1], scale=1.0),
                reads=[ka, "convb"], writes=[kd])
            if cb < 4 and t0 == 0:
                P.op("pool", lambda e, dst=dst: e.memset(dst[:, 0:112], 0.0), writes=[kd])

    def prep(bi):
        ti, t0, ntok, b = blocks[bi]
        sl, sl3, pb = ti % 2, ti % 3, bi % 2
        K = lambda k: "%s_%d" % (k, pb)
        B = {k: v[pb] for k, v in blk.items()}
        xtok = B["xtok"]
        for cb in range(4):
            P.op("pe", lambda e, cb=cb: e.transpose(pXT[:, cb * 128:(cb + 1) * 128],
                                                    xsT[sl][:, cb * ntok + b * 128:cb * ntok + (b + 1) * 128], C["ident32"]),
                 reads=["xsT%d_%d" % (sl, cb), "c_ident32"], writes=["pXT"])
        P.op("act", lambda e: e.copy(out=xtok, in_=pXT), reads=["pXT"], writes=[K("xtok")])
        P.op("pe", lambda e: e.transpose(pBt, BT16[sl][:, b * 128:(b + 1) * 128], C["ident16"]),
             reads=["BT16_%d" % sl, "c_ident16"], writes=["b1"])
        P.op("dve", lambda e: e.tensor_copy(out=B["Btok"], in_=pBt), reads=["b1"], writes=[K("Btok")])
        P.op("dve", lambda e: e.tensor_tensor(out=B["dtx"], in0=dtrb[sl3][:, b * 8:(b + 1) * 8], in1=dtb, op=ALU.add),
             reads=["dtr%d" % sl3, "dtb"], writes=[K("dtx")])
        P.op("act", lambda e: e.activation(out=B["dtx"], in_=B["dtx"], func=AF.Exp), reads=[K("dtx")], writes=[K("dtx")])
        P.op("act", lambda e: e.activation(out=B["dt"], in_=B["dtx"], func=AF.Ln, bias=1.0, scale=1.0),
             reads=[K("dtx")], writes=[K("dt")])
        P.op("dve", lambda e: e.tensor_tensor(out=B["a"], in0=B["dt"], in1=aneg, op=ALU.mult),
             reads=[K("dt"), "aneg"], writes=[K("a")])
        a = B["a"]
        P.op("pe", lambda e: e.matmul(pA[:, 0:8], C["ssd_tri"], a, start=True, stop=True), reads=[K("a"), "c_ssd_tri"], writes=["pA"])
        P.op("pe", lambda e: e.matmul(pA[:, 8:16], C["ssd_triu"], a, start=True, stop=True), reads=[K("a"), "c_ssd_triu"], writes=["pA"])
        for c in range(2):
            P.op("pe", lambda e, c=c: e.matmul(pA[:, 144 + 8 * c:152 + 8 * c], C["ssd_ch"][:, c * 128:(c + 1) * 128],
                                               a, start=True, stop=True),
                 reads=[K("a"), "c_ssd_ch"], writes=["pA"])
        P.op("act", lambda e: e.activation(out=B["eacum"], in_=pA[:, 0:8], func=AF.Exp), reads=["pA"], writes=[K("eacum")])
        P.op("act", lambda e: e.activation(out=B["erev"], in_=pA[:, 8:16], func=AF.Exp), reads=["pA"], writes=[K("erev")])
        P.op("act", lambda e: e.activation(out=B["dec"], in_=pA[:, 144:160], func=AF.Exp), reads=["pA"], writes=[K("dec")])
        P.op("dve", lambda e: e.tensor_scalar(out=B["nacum"], in0=pA[:, 0:8], scalar1=-1.0, scalar2=None, op0=ALU.mult),
             reads=["pA"], writes=[K("nacum")])
        P.op("dve", lambda e: e.tensor_tensor(out=B["R"].rearrange("p (h t) -> p h t", h=8),
                                              in0=B["a"].unsqueeze(2).broadcast_to([128, 8, 128]),
                                              in1=C["ssd_tri"].unsqueeze(1).broadcast_to([128, 8, 128]), op=ALU.mult),
             reads=[K("a"), "c_ssd_tri"], writes=[K("R")])
        P.op("dve", lambda e: e.tensor_tensor(out=B["de"], in0=B["dt"], in1=B["erev"], op=ALU.mult),
             reads=[K("dt"), K("erev")], writes=[K("de")])
        P.op("dve", lambda e: e.tensor_tensor(out=h3(B["xdt"]), in0=h3(xtok), in1=bc(B["dt"], 128), op=ALU.mult),
             reads=[K("xtok"), K("dt")], writes=[K("xdt")])
        P.op("pool", lambda e: e.tensor_tensor(out=h3(B["xdd"]), in0=h3(xtok), in1=bc(B["de"], 128), op=ALU.mult),
             reads=[K("xtok"), K("de")], writes=[K("xdd")])
        P.op("pool", lambda e: e.tensor_tensor(out=h3(B["xD"]), in0=h3(xtok), in1=bc(dsk, 128), op=ALU.mult),
             reads=[K("xtok"), "dsk"], writes=[K("xD")])
        P.op("pe", lambda e: e.matmul(pCB, BT16[sl][:, b * 128:(b + 1) * 128], CT16[sl][:, b * 128:(b + 1) * 128],
                                      start=True, stop=True), reads=["BT16_%d" % sl, "CT16_%d" % sl], writes=["b1"])
        P.op("act", lambda e: e.copy(out=B["cb"], in_=pCB), reads=["b1"], writes=[K("cb")])
        for half in range(2):
            kl = "pL%d" % half
            P.op("pe", lambda e, half=half: e.matmul(pL[half], C["ident16"], C["ssd_mneg4"], start=True, stop=False),
                 reads=["c_ident16", "c_ssd_mneg4"], writes=[kl])
            P.op("pe", lambda e, half=half: e.matmul(pL[half], C["ones32"], B["R"][:, half * 512:(half + 1) * 512],
                                                     start=False, stop=True), reads=["c_ones32", K("R")], writes=[kl])
            for h in range(half * 4, half * 4 + 4):
                pl = pL[half][:, (h % 4) * 128:(h % 4 + 1) * 128]
                P.op("act", lambda e, pl=pl, h=h: e.activation(out=B["Eall"][:, h * 128:(h + 1) * 128], in_=pl, func=AF.Exp,
                                                               bias=B["nacum"][:, h:h + 1], scale=1.0),
                     reads=[kl, K("nacum")], writes=[K("Eall") + "_%d" % h])
        P.op("dve", lambda e: e.tensor_tensor(out=B["M16"].rearrange("p (h t) -> p h t", h=8),
                                              in0=B["Eall"].rearrange("p (h t) -> p h t", h=8),
                                              in1=B["cb"].unsqueeze(1).broadcast_to([128, 8, 128]), op=ALU.mult),
             reads=[K("Eall") + "_%d" % h for h in range(8)] + [K("cb")], writes=[K("M16")])

    def scan(bi):
        ti, t0, ntok, b = blocks[bi]
        sl, sl3, pb = ti % 2, ti % 3, bi % 2
        K = lambda k: "%s_%d" % (k, pb)
        B = {k: v[pb] for k, v in blk.items()}
        ysb = B["ysb"]
        for h in range(8):
            P.op("pe", lambda e, h=h: e.matmul(pY[:, h * 64:(h + 1) * 64], B["M16"][:, h * 128:(h + 1) * 128],
                                               B["xdt"][:, h * 64:(h + 1) * 64], start=True, stop=True),
                 reads=[K("M16"), K("xdt")], writes=["pY"])
        for c in range(2):
            ci = 2 * bi + c
            cs = c * 64
            s16, ks16 = hs16[(ci + 1) % 2], "hs16_%d" % ((ci + 1) % 2)
            s16n, ks16n = hs16[ci % 2], "hs16_%d" % (ci % 2)
            P.op("pe", lambda e, s16=s16: e.matmul(pYo, CT16[sl][:, b * 128:(b + 1) * 128], s16, start=True, stop=True),
                 reads=["CT16_%d" % sl, ks16], writes=["pYo"])
            P.op("pe", lambda e, cs=cs: e.matmul(pU, B["Btok"][cs:cs + 64, :], B["xdd"][cs:cs + 64, :], start=True, stop=True),
                 reads=[K("Btok"), K("xdd")], writes=["pU"])
            P.op("dve", lambda e, cs=cs: e.tensor_tensor(out=h3(ysb[cs:cs + 64, :]), in0=h3(pYo[cs:cs + 64, :]),
                                                         in1=bc(B["eacum"][cs:cs + 64, :], 64), op=ALU.mult),
                 reads=["pYo", K("eacum")], writes=[K("ysb") + "_%d" % c])
            P.op("dve", lambda e, c=c: e.tensor_tensor(out=h3(hs32), in0=h3(hs32), in1=bc(B["dec"][:, 8 * c:8 * c + 8], 128),
                                                       op=ALU.mult), reads=["hs32", K("dec")], writes=["hs32"])
            P.op("dve", lambda e: e.tensor_tensor(out=hs32, in0=hs32, in1=pU, op=ALU.add), reads=["hs32", "pU"], writes=["hs32"])
            P.op("act", lambda e, s16n=s16n: e.copy(out=s16n, in_=hs32), reads=["hs32"], writes=[ks16n])
        kys = [K("ysb") + "_0", K("ysb") + "_1"]
        P.op("dve", lambda e: e.tensor_tensor(out=ysb, in0=ysb, in1=pY, op=ALU.add), reads=kys + ["pY"], writes=kys)
        P.op("pool", lambda e: e.tensor_tensor(out=ysb, in0=ysb, in1=B["xD"], op=ALU.add), reads=kys + [K("xD")], writes=kys)
        P.op("pool", lambda e: e.tensor_tensor(out=yz, in0=ysb, in1=sztb[sl3][:, b * 512:(b + 1) * 512], op=ALU.mult),
             reads=kys + ["szt%d" % sl3], writes=["yz"])
        P.op("act", lambda e: e.activation(out=ysq, in_=yz, func=AF.Square, accum_out=ssq), reads=["yz"], writes=["ysq", "ssq"])
        P.op("act", lambda e: e.activation(out=rstd, in_=ssq, func=AF.Ln, bias=RMS_EPS, scale=1.0 / 512.0),
             reads=["ssq"], writes=["rstd"])
        P.op("act", lambda e: e.activation(out=rstd, in_=rstd, func=AF.Exp, scale=-0.5), reads=["rstd"], writes=["rstd"])
        P.op("dve", lambda e: e.scalar_tensor_tensor(out=yn16, in0=yz, scalar=rstd, in1=nwb, op0=ALU.mult, op1=ALU.mult),
             reads=["yz", "rstd", "nwb"], writes=["yn16"])
        for cb in range(4):
            P.op("pe", lambda e, cb=cb: e.transpose(pT[:, cb * 128:(cb + 1) * 128], yn16[:, cb * 128:(cb + 1) * 128], C["ident16"]),
                 reads=["yn16", "c_ident16"], writes=["b1"])
        o16, kos = ost[ti % 2], "sost%d" % (ti % 2)
        P.op("dve", lambda e: e.tensor_copy(out=o16[:, 0:4 * ntok].rearrange("p (j t) -> p j t", j=4)[:, :, b * 128:(b + 1) * 128],
                                            in_=pT.rearrange("p (j t) -> p j t", j=4)), reads=["b1"], writes=[kos + "_%d" % b])
        if b == ntok // 128 - 1:
            nb = ntok // 128
            P.dma("pool", mix_d[:, :, t0:t0 + ntok].rearrange("j p t -> p j t"),
                  o16[:, 0:4 * ntok].rearrange("p (j t) -> p j t", j=4), reads=[kos + "_%d" % x for x in range(nb)],
                  chan="st_sost%d" % (ti % 2))

    load_tile(0)
    if len(tiles) > 1:
        load_tile(1)
    conv(0)
    prep(0)
    for bi in range(len(blocks)):
        if bi + 1 < len(blocks):
            nti, _, _, nb_ = blocks[bi + 1]
            if nb_ == 0:
                conv(nti)
                if nti + 1 < len(tiles):
                    load_tile(nti + 1)
            prep(bi + 1)
        scan(bi)
    P.op("sp", None, writes=["sost%d_%d" % (i, x) for i in range(2) for x in range(4)])


def build_even_mixer(NB):
    nc = bass.Bass("TRN2", target_bir_lowering=False)
    cx = Cx(nc)
    T = NB * 128
    hT_d = cx.dram("hT", [16, 128, T], BF16, "ExternalInput")
    W_d = cx.dram("W", [D, E_NC], F32, "ExternalInput")
    waug_d = cx.dram("waug", [17, 256], F32, "ExternalInput")
    normw_d = cx.dram("normw", [128, 2], F32, "ExternalInput")
    mix_gla = cx.dram("mix_gla", [2, 128, T], BF16, "ExternalOutput")
    mix_sb = cx.dram("mix_sb", [128, T], BF16, "ExternalOutput")
    ss = cx.dram("ss_out", [1, T], F32, "ExternalOutput")
    S = even_scratch(cx, NB)
    emit_even_inproj(cx, NB, hT_d, W_d, S)
    cx.P.barrier()
    cx.off = 0
    emit_gla(cx, NB, S, waug_d, normw_d, mix_gla, ss)
    cx.P.barrier()
    cx.off = 0
    emit_sb(cx, NB, S, mix_sb)
    cx.P.emit()
    return nc


def build_odd_mixer(NB):
    nc = bass.Bass("TRN2", target_bir_lowering=False)
    cx = Cx(nc)
    T = NB * 128
    hT_d = cx.dram("hT", [16, 128, T], BF16, "ExternalInput")
    W_d = cx.dram("W", [D, O_NC], F32, "ExternalInput")
    prm = {"convw": cx.dram("convw", [128, 24], F32, "ExternalInput"), "convb": cx.dram("convb", [128, 6], F32, "ExternalInput"),
           "dtb": cx.dram("dtb", [1, 8], F32, "ExternalInput"), "alog": cx.dram("alog", [1, 8], F32, "ExternalInput"),
           "dskip": cx.dram("dskip", [1, 8], F32, "ExternalInput"), "normw": cx.dram("normw", [1, 512], F32, "ExternalInput")}
    mix = cx.dram("mix_ssd", [4, 128, T], BF16, "ExternalOutput")
    S = odd_scratch(cx, NB)
    emit_odd_inproj(cx, NB, hT_d, W_d, S)
    cx.P.barrier()
    cx.off = 0
    emit_ssd(cx, NB, S, prm, mix)
    cx.P.emit()
    return nc


_CONSTS = None
EVEN_GROUPS = [(4, 0), (4, 1), (4, 2), (4, 3), (8, None)]
ODD_GROUPS = [(32, None)]


def _cin(names):
    global _CONSTS
    if _CONSTS is None:
        _CONSTS = make_consts()
    return {"c_" + n: _CONSTS[n] for n in names}


def _const_names(nc):
    out = []
    for a in nc.allocations:
        if isinstance(a, mybir.MemoryLocationSet) and a.kind == "ExternalInput":
            nm = a.memorylocations[0].name
            if nm.startswith("c_"):
                out.append(nm[2:])
    return out


def _run(nc, in_maps):
    cn = _cin(_const_names(nc))
    maps = [dict(m, **cn) for m in in_maps]
    return run_bass_kernel_spmd(nc, maps, core_ids=list(range(len(maps)))).results


def kernel(x, meta, ev_w_in, ev_gla_w_gate2, ev_gla_b_gate, ev_gla_norm_w, ev_w_out,
           od_w_in, od_conv_w, od_conv_b, od_dt_bias, od_a_log, od_d_skip, od_norm_w,
           od_w_out, ln_g, ln_b):
    f32 = np.float32
    x = np.asarray(x, f32)
    SEQ = x.shape[1]
    NB = SEQ // 128 + 1
    T = NB * 128
    NC = 8
    TPC = (NB - 1) // NC
    NT = 1 + TPC
    TT = NT * 128
    pre = np.zeros((128, D), f32)
    pre[112:] = np.asarray(meta, f32)

    def tok_cols(c):
        return np.concatenate([np.arange(128), 128 + c * TPC * 128 + np.arange(TPC * 128)])

    h_cur = [np.concatenate([pre, x[0, c * TPC * 128:(c + 1) * TPC * 128]], 0) for c in range(NC)]
    nc0 = build_p3(NT, None, False)
    r0 = _run(nc0, [{"h_in": h_cur[c]} for c in range(NC)])

    def gather_hT(rs):
        hT = np.empty((16, 128, T), NPBF)
        hT[:, :, 0:128] = rs[0]["hT_out"][:, :, 0:128]
        for c in range(NC):
            hT[:, :, 128 + c * TPC * 128:128 + (c + 1) * TPC * 128] = rs[c]["hT_out"][:, :, 128:]
        return hT

    hT = gather_hT(r0)
    nce = build_even_mixer(NB)
    nco = build_odd_mixer(NB)
    ncp_e = build_p3(NT, EVEN_GROUPS, True)
    ncp_o = build_p3(NT, ODD_GROUPS, True)
    for layer in range(DEPTH):
        j = layer // 2
        if layer % 2 == 0:
            w = np.asarray(ev_w_in[j], f32)
            off = np.cumsum([0, 1024, 1024, 2048, 2048, 16, 1024, 1024, 1024, 1024])
            maps = []
            for c in range(NC):
                g, s_ = c // 2, c % 2
                cols = np.concatenate([
                    off[0] + g * 256 + np.arange(256), off[1] + g * 256 + np.arange(256),
                    off[2] + g * 512 + s_ * 256 + np.arange(256), off[3] + g * 512 + s_ * 256 + np.arange(256),
                    off[4] + np.arange(16), off[5] + c * 128 + np.arange(128), off[6] + c * 128 + np.arange(128),
                    off[7] + c * 128 + np.arange(128), off[8] + c * 128 + np.arange(128)])
                waug = np.concatenate([np.asarray(ev_gla_w_gate2[j], f32)[:, g * 256:(g + 1) * 256],
                                       np.asarray(ev_gla_b_gate[j], f32)[None, g * 256:(g + 1) * 256]], 0)
                nw = np.asarray(ev_gla_norm_w[j], f32)[g * 512 + s_ * 256:g * 512 + (s_ + 1) * 256]
                maps.append({"hT": hT, "W": np.ascontiguousarray(w[:, cols]), "waug": np.ascontiguousarray(waug),
                             "normw": np.ascontiguousarray(nw.reshape(2, 128).T)})
            ra = _run(nce, maps)
            mixT = np.empty((24, 128, T), NPBF)
            ssf = np.empty((8, T), f32)
            for c in range(NC):
                mixT[2 * c:2 * c + 2] = ra[c]["mix_gla"]
                mixT[16 + c] = ra[c]["mix_sb"]
                ssf[c] = ra[c]["ss_out"][0]
            wout = np.asarray(ev_w_out[j], f32)
            ncp = ncp_e
        else:
            w = np.asarray(od_w_in[j], f32)
            cw = np.asarray(od_conv_w[j], f32)
            cb = np.asarray(od_conv_b[j], f32)
            maps = []
            for c in range(NC):
                cols = np.concatenate([c * 512 + np.arange(512), 4096 + c * 512 + np.arange(512),
                                       8192 + c * 128 + np.arange(128), 9216 + c * 128 + np.arange(128),
                                       10240 + c * 8 + np.arange(8)])
                cch = np.concatenate([c * 512 + np.arange(512), 4096 + c * 128 + np.arange(128), 5120 + c * 128 + np.arange(128)])
                maps.append({"hT": hT, "W": np.ascontiguousarray(w[:, cols]),
                             "convw": np.ascontiguousarray(cw[:, cch].T.reshape(6, 128, 4).transpose(1, 0, 2).reshape(128, 24)),
                             "convb": np.ascontiguousarray(cb[cch].reshape(6, 128).T),
                             "dtb": np.asarray(od_dt_bias[j], f32)[None, c * 8:(c + 1) * 8],
                             "alog": np.asarray(od_a_log[j], f32)[None, c * 8:(c + 1) * 8],
                             "dskip": np.asarray(od_d_skip[j], f32)[None, c * 8:(c + 1) * 8],
                             "normw": np.asarray(od_norm_w[j], f32)[None, c * 512:(c + 1) * 512]})
            ra = _run(nco, maps)
            mixT = np.empty((32, 128, T), NPBF)
            for c in range(NC):
                mixT[4 * c:4 * c + 4] = ra[c]["mix_ssd"]
            ssf = None
            wout = np.asarray(od_w_out[j], f32)
            ncp = ncp_o
        maps = []
        for c in range(NC):
            tc = tok_cols(c)
            m = {"h_in": h_cur[c], "mixT": np.ascontiguousarray(mixT[:, :, tc]), "wout": wout,
                 "lng": np.asarray(ln_g[layer], f32)[None], "lnb": np.asarray(ln_b[layer], f32)[None]}
            if ssf is not None:
                m["ss"] = np.ascontiguousarray(ssf[:, tc])
            maps.append(m)
        rb = _run(ncp, maps)
        h_cur = [rb[c]["h_out"] for c in range(NC)]
        hT = gather_hT(rb)
    out = np.concatenate([h_cur[c][128:] for c in range(NC)], 0)[None]
    return out.astype(f32)
```
